# Optimizing a Trainium2 kernel written in Bass

```python
import math
import jax, jax.numpy as jnp
from jax import lax
import numpy as np

D_MODEL = 1024
BATCH = 4
SEQ = 8192
DEPTH = 4

N_M = 4
HD_M = 128
MW = N_M * HD_M
MLSTM_CHUNK = 128
CONV_W = 5
N_A = 4
HD_A = 64
AW = N_A * 2 * HD_A
D_MIX = MW + AW
Q_BLOCK = 128
REL_BUCKETS = 32
REL_MAX_DIST = 128
D_FF = 4 * D_MODEL
EPS = 1e-6
C_QK_M = 0
C_V_M = 2 * MW
C_O_M = 3 * MW
C_G = 4 * MW
C_Q_A = C_G + 4 * N_M
C_K_A = C_Q_A + AW
C_V_A = C_K_A + AW
D_IN = C_V_A + AW

kernel_name = 'hybrid_mlstm_diffattn_block'


def rms(x, g):
    xf = x.astype(jnp.float32)
    return xf * lax.rsqrt(jnp.mean(xf * xf, axis=-1, keepdims=True) + EPS) * g.astype(jnp.float32)


def t5_bucket(rel):
    nb = REL_BUCKETS // 2
    max_exact = nb // 2
    n = jnp.abs(rel)
    is_small = n < max_exact
    nf = jnp.maximum(n, 1).astype(jnp.float32)
    large = max_exact + (jnp.log(nf / max_exact) / math.log(REL_MAX_DIST / max_exact) * (nb - max_exact)).astype(jnp.int32)
    large = jnp.minimum(large, nb - 1)
    return jnp.where(rel > 0, nb, 0) + jnp.where(is_small, n, large)


def dwconv_centred(x, w, b):
    C = x.shape[-1]
    y = lax.conv_general_dilated(x, w[:, None, :].astype(x.dtype), window_strides=(1,), padding='SAME',
                                 dimension_numbers=('NWC', 'WIO', 'NWC'), feature_group_count=C)
    return y + b.astype(x.dtype)


def to_heads(t, H, d):
    B, S, _ = t.shape
    return t.reshape(B, S, H, d).transpose(0, 2, 1, 3)


def mlstm_chunkwise(q, k, v, log_i, log_f):
    B, H, S, d = q.shape
    L = MLSTM_CHUNK
    nc = S // L

    def chunks(t):
        return jnp.moveaxis(t.reshape(B, H, nc, L, *t.shape[3:]), 2, 0)

    tri = jnp.tril(jnp.ones((L, L), dtype=bool))

    def step(carry, inp):
        C, n, m = carry
        qc, kc, vc, ic, fc = inp
        b = jnp.cumsum(fc, axis=-1)
        D = jnp.where(tri, b[..., :, None] - b[..., None, :] + ic[..., None, :], -jnp.inf)
        inter = b + m[..., None]
        m_t = jnp.maximum(inter, jnp.max(D, axis=-1))
        scale = jnp.exp(inter - m_t)
        wqk = jnp.einsum('bhtd,bhsd->bhts', qc, kc) * jnp.exp(D - m_t[..., None])
        num = scale[..., None] * jnp.einsum('bhtk,bhkv->bhtv', qc, C) + jnp.einsum('bhts,bhsv->bhtv', wqk, vc)
        den = scale * jnp.einsum('bhtk,bhk->bht', qc, n) + jnp.sum(wqk, axis=-1)
        h = num / jnp.maximum(jnp.abs(den), jnp.exp(-m_t))[..., None]
        bL = b[..., -1]
        g = bL[..., None] - b + ic
        m_new = jnp.maximum(bL + m, jnp.max(g, axis=-1))
        decay = jnp.exp(bL + m - m_new)
        ws = jnp.exp(g - m_new[..., None])
        C_new = decay[..., None, None] * C + jnp.einsum('bhs,bhsk,bhsv->bhkv', ws, kc, vc)
        n_new = decay[..., None] * n + jnp.einsum('bhs,bhsk->bhk', ws, kc)
        return (C_new, n_new, m_new), h

    init = (jnp.zeros((B, H, d, d), jnp.float32), jnp.zeros((B, H, d), jnp.float32), jnp.zeros((B, H), jnp.float32))
    _, hs = lax.scan(step, init, (chunks(q), chunks(k), chunks(v), chunks(log_i), chunks(log_f)))
    return jnp.moveaxis(hs, 0, 2).reshape(B, H, S, d)


def mlstm_group(qk, v, o, gates, gate_b, norm_g):
    B, S, _ = v.shape
    q = to_heads(qk[..., :MW], N_M, HD_M).astype(jnp.float32)
    k = to_heads(qk[..., MW:], N_M, HD_M).astype(jnp.float32) * (HD_M ** -0.5)
    vh = to_heads(v, N_M, HD_M).astype(jnp.float32)
    g = gates.astype(jnp.float32).reshape(B, S, 4, N_M) + gate_b.astype(jnp.float32)
    g = g.transpose(2, 0, 3, 1)
    hf = mlstm_chunkwise(q, k, vh, g[0], jax.nn.log_sigmoid(g[1]))
    fl = lambda t: jnp.flip(t, axis=2)
    hb = fl(mlstm_chunkwise(fl(q), fl(k), fl(vh), fl(g[2]), fl(jax.nn.log_sigmoid(g[3]))))
    hs = rms(hf + hb, norm_g.reshape(N_M, 1, HD_M))
    hs = hs.transpose(0, 2, 1, 3).reshape(B, S, MW)
    return jax.nn.sigmoid(o.astype(jnp.float32)) * hs


def diff_attention_group(qa, ka, va, q_g, k_g, lam_vec, lam_init, rel_bias, sub_g):
    B, S, _ = qa.shape
    q = qa.reshape(B, S, N_A, 2, HD_A).transpose(0, 3, 2, 1, 4)
    k = ka.reshape(B, S, N_A, 2, HD_A).transpose(0, 3, 2, 1, 4)
    q = rms(q, q_g) * (HD_A ** -0.5)
    k = rms(k, k_g)
    vh = va.reshape(B, S, N_A, 2 * HD_A).transpose(0, 2, 1, 3).astype(jnp.float32)
    lv = lam_vec.astype(jnp.float32)
    lam = jnp.exp(jnp.sum(lv[0] * lv[1])) - jnp.exp(jnp.sum(lv[2] * lv[3])) + lam_init
    nq = S // Q_BLOCK
    qb = q.reshape(B, 2, N_A, nq, Q_BLOCK, HD_A).transpose(3, 0, 1, 2, 4, 5)
    starts = jnp.arange(nq, dtype=jnp.int32) * Q_BLOCK
    kpos = jnp.arange(S, dtype=jnp.int32)
    table = rel_bias.astype(jnp.float32)

    def block(args):
        qblk, start = args
        qpos = start + jnp.arange(Q_BLOCK, dtype=jnp.int32)
        bias = table[t5_bucket(kpos[None, :] - qpos[:, None])].transpose(2, 0, 1)
        logits = jnp.einsum('bmhqd,bmhkd->bmhqk', qblk, k) + bias
        p = jax.nn.softmax(logits, axis=-1)
        a = p[:, 0] - lam * p[:, 1]
        return jnp.einsum('bhqk,bhkv->bhqv', a, vh)

    out = lax.map(block, (qb, starts))
    out = out.transpose(1, 2, 0, 3, 4).reshape(B, N_A, S, 2 * HD_A)
    out = rms(out, sub_g.reshape(N_A, 1, 2 * HD_A)) * (1.0 - lam_init)
    return out.transpose(0, 2, 1, 3).reshape(B, S, AW)


def setup_inputs(seed: int = 0) -> dict:
    key = jax.random.key(seed)
    ks = jax.random.split(key, 20)
    f32 = jnp.float32
    nrm = lambda k, s, sc: jax.random.normal(k, s, f32) * sc
    forget_base = jnp.array([0.0, 1.0, 0.0, 1.0], f32)[:, None] * jnp.linspace(3.0, 6.0, N_M, dtype=f32)[None, :]
    return {
        'x': nrm(ks[0], (BATCH, SEQ, D_MODEL), 1.0),
        'norm1_g': 1.0 + nrm(ks[1], (DEPTH, D_MODEL), 0.02),
        'w_in': nrm(ks[2], (DEPTH, D_MODEL, D_IN), D_MODEL ** -0.5),
        'conv_w': nrm(ks[3], (DEPTH, CONV_W, 2 * MW), CONV_W ** -0.5),
        'conv_b': nrm(ks[4], (DEPTH, 2 * MW), 0.01),
        'gate_b': forget_base[None] + nrm(ks[5], (DEPTH, 4, N_M), 0.1),
        'mlstm_norm_g': 1.0 + nrm(ks[6], (DEPTH, MW), 0.02),
        'q_norm_g': 1.0 + nrm(ks[7], (DEPTH, HD_A), 0.02),
        'k_norm_g': 1.0 + nrm(ks[8], (DEPTH, HD_A), 0.02),
        'lambdas': nrm(ks[9], (DEPTH, 4, HD_A), 0.1),
        'diff_norm_g': 1.0 + nrm(ks[10], (DEPTH, AW), 0.02),
        'rel_bias': nrm(ks[11], (REL_BUCKETS, N_A), 0.5),
        'w_out': nrm(ks[12], (DEPTH, D_MIX, D_MODEL), D_MIX ** -0.5),
        'norm2_g': 1.0 + nrm(ks[13], (DEPTH, D_MODEL), 0.02),
        'w_ff1': nrm(ks[14], (DEPTH, D_MODEL, D_FF), D_MODEL ** -0.5),
        'w_ff2': nrm(ks[15], (DEPTH, D_FF, D_MODEL), D_FF ** -0.5),
    }


def reference(x, norm1_g, w_in, conv_w, conv_b, gate_b, mlstm_norm_g, q_norm_g, k_norm_g, lambdas,
              diff_norm_g, rel_bias, w_out, norm2_g, w_ff1, w_ff2):
    dt = x.dtype
    for l in range(DEPTH):
        lam_init = 0.8 - 0.6 * math.exp(-0.3 * l)
        h = rms(x, norm1_g[l]).astype(dt)
        p = h @ w_in[l]
        qk_m = jax.nn.silu(dwconv_centred(p[..., C_QK_M:C_V_M], conv_w[l], conv_b[l]))
        mix_a = mlstm_group(qk_m, p[..., C_V_M:C_O_M], p[..., C_O_M:C_G], p[..., C_G:C_Q_A],
                            gate_b[l], mlstm_norm_g[l])
        mix_b = diff_attention_group(p[..., C_Q_A:C_K_A], p[..., C_K_A:C_V_A], p[..., C_V_A:D_IN],
                                     q_norm_g[l], k_norm_g[l], lambdas[l], lam_init, rel_bias, diff_norm_g[l])
        mixed = jnp.concatenate([mix_a, mix_b], axis=-1).astype(dt)
        x = x + (mixed @ w_out[l]).astype(dt)
        u = rms(x, norm2_g[l]).astype(dt) @ w_ff1[l]
        u = jnp.square(jax.nn.relu(u))
        x = x + (u @ w_ff2[l]).astype(dt)
    return x
```

```python
import math
import numpy as np
from contextlib import ExitStack
import concourse.bass as bass
import concourse.mybir as mybir
from concourse.bass_utils import run_bass_kernel_spmd

F32 = mybir.dt.float32
BF16 = mybir.dt.bfloat16
AF = mybir.ActivationFunctionType
ALU = mybir.AluOpType

D_MODEL = 1024
BATCH = 4
SEQ = 8192
DEPTH = 4
N_M = 4
MW = 512
AW = 512
D_FF = 4096
EPS = 1e-6
C_QK_M = 0
C_V_M = 1024
C_O_M = 1536
C_G = 2048
C_Q_A = 2064
C_K_A = 2576
C_V_A = 3088
D_IN = 3600
NCH = SEQ // 128
NT = SEQ // 512
HALF = SEQ // 2
ENGS = ("pe", "act", "dve", "pool", "sp")


class _Rec:
    def __init__(self):
        self.call = None

    def __getattr__(self, name):
        def f(*a, **kw):
            assert self.call is None
            self.call = (name, a, kw)
        return f


class Prog:
    def __init__(self, nc):
        self.nc = nc
        self.ops = []
        self.stack = ExitStack()
        self.last_w = {}
        self.readers = {}
        self.dma_sem_names = []
        self.pool_ctr = {}
        self.grp_pending = {}
        self.grp_touched = {}
        self.barrier_idx = None
        self.stream_last = {}

    def barrier(self):
        deps = set(self.stream_last.values())
        idx = len(self.ops)
        rec = ("engine_nop", (), {})
        self.ops.append(dict(eng="pool", fn=rec, deps=deps, dsem=None, observed=False))
        self.barrier_idx = idx
        self.stream_last = {"pool": idx}
        self.last_w = {}
        self.readers = {}
        self.grp_pending = {}
        self.grp_touched = {}
        return idx

    def sb(self, name, shape, dt):
        return self.stack.enter_context(self.nc.sbuf_tensor(name, list(shape), dt))

    def ps(self, name, shape, dt=F32):
        return self.stack.enter_context(self.nc.psum_tensor(name, list(shape), dt))

    def dma_sem(self, name):
        self.dma_sem_names.append(name)
        return ("dma", name)

    def regroup(self, base):
        self.grp_pending[base] = self.grp_touched.get(base, {})
        self.grp_touched[base] = {}

    def op(self, eng, fn, r=(), w=(), dsem=None):
        idx = len(self.ops)
        strm = dsem if dsem is not None else eng
        r = list(r)
        w = list(w)
        for k in list(r):
            if isinstance(k, tuple) and len(k) >= 3 and k[1] == "bank":
                r.remove(k)
                if k not in w:
                    w.append(k)
        deps = set()
        for k in r:
            if k in self.last_w:
                deps.add(self.last_w[k])
        for k in w:
            if k in self.last_w:
                deps.add(self.last_w[k])
            rd = self.readers.get(k)
            if rd:
                deps.update(rd.values())
        for k in tuple(r) + tuple(w):
            if isinstance(k, tuple) and len(k) >= 3 and k[1] == "bank":
                base = k[:3]
                pend = self.grp_pending.get(base)
                if pend:
                    deps.update(pend.values())
                self.grp_touched.setdefault(base, {})[strm] = idx
        for k in r:
            self.readers.setdefault(k, {})[strm] = idx
        for k in w:
            self.last_w[k] = idx
            self.readers[k] = {}
        if self.barrier_idx is not None:
            deps.add(self.barrier_idx)
        if eng == "pe" and dsem is None:
            deps = {d for d in deps if not (self.ops[d]["eng"] == "pe" and self.ops[d]["dsem"] is None)}
        self.stream_last[strm] = idx
        rec = _Rec()
        fn(rec)
        assert rec.call is not None
        self.ops.append(dict(eng=eng, fn=rec.call, deps=deps, dsem=dsem, observed=False))
        return idx

    def emit(self):
        import os
        nc = self.nc
        mx = int(os.environ.get("PROG_MAXOPS", "0"))
        if mx:
            self.ops = self.ops[:mx]
        ops = self.ops

        def stream(o):
            return o["dsem"] if o["dsem"] is not None else o["eng"]

        for o in ops:
            for d in o["deps"]:
                ops[d]["observed"] = True
        lastop = {}
        for i, o in enumerate(ops):
            if o["dsem"] is None:
                lastop[o["eng"]] = i
        for e_, i in lastop.items():
            ops[i]["observed"] = True
        cnt = {}
        for o in ops:
            s = stream(o)
            if o["dsem"] is not None:
                cnt[s] = cnt.get(s, 0) + 16
                o["val"] = cnt[s]
            elif o["observed"]:
                cnt[s] = cnt.get(s, 0) + 1
                o["val"] = cnt[s]
            else:
                o["val"] = None
        sems = {}
        for e in ENGS:
            sems[e] = self.stack.enter_context(nc.semaphore("s_" + e))
        for nm in self.dma_sem_names:
            sems[("dma", nm)] = self.stack.enter_context(nc.semaphore("d_" + nm))
        per_eng = {e: [] for e in ENGS}
        for i, o in enumerate(ops):
            per_eng[o["eng"]].append(i)
        block = self.stack.enter_context(nc.Block())
        engobj = {"pe": "tensor", "act": "scalar", "dve": "vector", "pool": "gpsimd", "sp": "sync"}
        dma_names = self.dma_sem_names
        self.n_waits = 0

        def make(e):
            def body(eng):
                waited = {}
                for i in per_eng[e]:
                    o = ops[i]
                    need = {}
                    for d in o["deps"]:
                        po = ops[d]
                        s = stream(po)
                        v = po["val"]
                        if need.get(s, 0) < v:
                            need[s] = v
                    for s, v in need.items():
                        if waited.get(s, 0) >= v:
                            continue
                        eng.wait_ge(sems[s], v)
                        self.n_waits += 1
                        waited[s] = v
                    nm_, a_, kw_ = o["fn"]
                    ins = getattr(eng, nm_)(*a_, **kw_)
                    if o["dsem"] is not None:
                        ins.then_inc(sems[o["dsem"]], 16)
                    elif o["observed"]:
                        ins.then_inc(sems[e], 1)
                if e == "sp":
                    for e2 in ENGS:
                        if e2 != "sp" and e2 in cnt:
                            eng.wait_ge(sems[e2], cnt[e2])
                    for nm in dma_names:
                        s = ("dma", nm)
                        if s in cnt:
                            eng.wait_ge(sems[s], cnt[s])
            return body

        for e in ENGS:
            if per_eng[e] or e == "sp":
                getattr(block, engobj[e])(make(e))
        self.stack.close()


def _t5_bucket_np(rel):
    nb = 16
    max_exact = 8
    n = np.abs(rel)
    is_small = n < max_exact
    nf = np.maximum(n, 1).astype(np.float32)
    large = max_exact + (np.log(nf / max_exact) / math.log(128 / max_exact) * (nb - max_exact)).astype(np.int32)
    large = np.minimum(large, nb - 1)
    return np.where(rel > 0, nb, 0) + np.where(is_small, n, large)


class ParLayout:
    def __init__(self):
        self.off = {}
        self.n = 0

    def add(self, name, width):
        self.off[name] = (self.n, width)
        self.n += width


def par_layout():
    L = ParLayout()
    L.add("cw", 2 * 2 * 5)
    L.add("cb", 2 * 2)
    L.add("gb", 2 * 4)
    L.add("ng", 2 * 128)
    L.add("gq", 1)
    L.add("gk", 1)
    L.add("lv", 4 * 64)
    L.add("li", 2)
    L.add("rb", 2 * 2)
    L.add("dg", 2 * 128)
    L.add("ident", 128)
    L.add("maskf", 128)
    L.add("maskb", 128)
    L.add("blk", 128)
    L.add("ones", 128)
    return L


PL = par_layout()


def build_par(inp, l, g):
    par = np.zeros((128, PL.n), np.float32)

    def put(name, arr):
        o, w = PL.off[name]
        par[:, o:o + w] = np.asarray(arr, np.float32).reshape(128, w)

    cw = np.zeros((128, 2, 2, 5), np.float32)
    cb = np.zeros((128, 2, 2), np.float32)
    gb = np.zeros((128, 2, 4), np.float32)
    ng = np.zeros((128, 2, 128), np.float32)
    rb = np.zeros((128, 2, 2), np.float32)
    dg = np.zeros((128, 2, 128), np.float32)
    for j in range(2):
        h = 2 * g + j
        cw[:, j, 0, :] = inp["conv_w"][l][:, h * 128:(h + 1) * 128].T
        cw[:, j, 1, :] = inp["conv_w"][l][:, 512 + h * 128:512 + (h + 1) * 128].T
        cb[:, j, 0] = inp["conv_b"][l][h * 128:(h + 1) * 128]
        cb[:, j, 1] = inp["conv_b"][l][512 + h * 128:512 + (h + 1) * 128]
        gb[:, j, :] = inp["gate_b"][l][:, h][None, :]
        ng[:, j, :] = inp["mlstm_norm_g"][l][h * 128:(h + 1) * 128][None, :]
        rb[:, j, 0] = inp["rel_bias"][15, h]
        rb[:, j, 1] = inp["rel_bias"][31, h]
        dg[:, j, :] = inp["diff_norm_g"][l][h * 128:(h + 1) * 128][None, :]
    put("cw", cw)
    put("cb", cb)
    put("gb", gb)
    put("ng", ng)
    put("gq", np.tile(inp["q_norm_g"][l], 2)[:, None])
    put("gk", np.tile(inp["k_norm_g"][l], 2)[:, None])
    put("lv", np.broadcast_to(inp["lambdas"][l].reshape(1, 256), (128, 256)))
    lam_init = 0.8 - 0.6 * math.exp(-0.3 * l)
    put("li", np.broadcast_to(np.array([lam_init, 1.0 - lam_init], np.float32)[None, :], (128, 2)))
    put("rb", rb)
    put("dg", dg)
    put("ident", np.eye(128, dtype=np.float32))
    put("maskf", np.triu(np.ones((128, 128), np.float32)))
    put("maskb", np.tril(np.ones((128, 128), np.float32)))
    blk = np.zeros((128, 128), np.float32)
    blk[:64, :64] = 1.0
    blk[64:, 64:] = 1.0
    put("blk", blk)
    put("ones", np.ones((128, 128), np.float32))
    return par


_BT_IDX = None


def build_bt(inp, g):
    global _BT_IDX
    if _BT_IDX is None:
        k = np.arange(128)[:, None, None]
        i = np.arange(6)[None, :, None]
        q = np.arange(512)[None, None, :]
        _BT_IDX = _t5_bucket_np((i - 1) * 128 + k - q)
    bt = np.zeros((128, 2, 6, 512), np.float32)
    for j in range(2):
        bt[:, j] = inp["rel_bias"][:, 2 * g + j][_BT_IDX]
    return bt


def build_wA(inp, l, g):
    w = inp["w_in"][l]
    cols = []
    for j in range(2):
        h = 2 * g + j
        cols += list(range(C_QK_M + h * 128, C_QK_M + (h + 1) * 128))
        cols += list(range(C_QK_M + 512 + h * 128, C_QK_M + 512 + (h + 1) * 128))
        cols += list(range(C_V_M + h * 128, C_V_M + (h + 1) * 128))
        cols += list(range(C_O_M + h * 128, C_O_M + (h + 1) * 128))
        cols += [C_G + gt * 4 + h for gt in range(4)]
    for j in range(2):
        h = 2 * g + j
        cols += list(range(C_Q_A + h * 128, C_Q_A + (h + 1) * 128))
        cols += list(range(C_K_A + h * 128, C_K_A + (h + 1) * 128))
        cols += list(range(C_V_A + h * 128, C_V_A + (h + 1) * 128))
    return np.ascontiguousarray(w[:, cols])


class PhaseA:
    def __init__(self, P, hT, wA, par_d, bt_d, mixT_d, heads=(0, 1, 2, 3), tag="A", carve=None,
                 rows_m=0, rows_a=256):
        self.P = P
        self.hT, self.wA, self.par_d, self.bt_d, self.mixT_d = hT, wA, par_d, bt_d, mixT_d
        self.tag = tag
        self.heads = heads
        self.carve = carve
        self.rows_m, self.rows_a = rows_m, rows_a
        self.alloc()

    def bind(self, hT, wA, par_d, bt_d, mixT_d, rows_m, rows_a):
        self.hT, self.wA, self.par_d, self.bt_d, self.mixT_d = hT, wA, par_d, bt_d, mixT_d
        self.rows_m, self.rows_a = rows_m, rows_a

    def big(self, name, n32):
        if self.carve is not None:
            return self.carve(n32)
        return self.P.sb(self.tag + name, [128, n32], F32)[:, :]

    def k(self, *a):
        return (self.tag,) + a

    def alloc(self):
        P = self.P
        t = self.tag
        self.par = P.sb(t + "par", [128, PL.n], F32)
        self.Wbf = self.big("Wbf", 2064).bitcast(BF16).rearrange("p (k n) -> p k n", n=516)
        self.hTt = [self.big("hTt%d" % i, 2048).bitcast(BF16).rearrange("p (k n) -> p k n", n=512)
                    for i in range(2)]
        self.qbf = self.big("qbf", 4096).bitcast(BF16)
        self.kbf = self.big("kbf", 4096).bitcast(BF16)
        self.Vaug = self.big("Vaug", 4128).bitcast(BF16).rearrange("p (c n) -> p c n", n=129)
        self.mixT = self.big("mixT", 4096).bitcast(BF16)
        self.identb = P.sb(t + "identb", [128, 128], BF16)
        self.arena = self.big("arena", 15616)
        self.banks = [P.ps(t + "bank%d" % i, [128, 512], F32) for i in range(8)]
        self.pools = {"gen": list(range(8)), "acc": [0, 1, 2], "sc": [3, 4, 5, 6], "tp": [7]}
        self.small = P.sb(t + "small", [128, 64], F32)
        self.Rt = [P.sb(t + "R%d" % i, [64, 128], F32) for i in range(4)]
        self.Rtmp = [P.sb(t + "Rtmp%d" % i, [64, 128], F32) for i in range(6)]
        self.Rcol = P.sb(t + "Rcol", [64, 16], F32)
        self.rows = P.sb(t + "rows", [1, 512], F32)
        self.WS = [P.sb(t + "WS%d" % d, [128, NCH], F32) for d in range(2)]
        self.RR = [P.sb(t + "RR%d" % d, [128, NCH], F32) for d in range(2)]
        self.DEC = [P.sb(t + "DEC%d" % d, [128, NCH + 1], F32) for d in range(2)]
        self.diag = P.sb(t + "diag", [64, 64], F32)
        self.G = P.sb(t + "G", [128, NCH, 4], F32)
        self.Cd32 = [P.sb(t + "Cd32_%d" % d, [128, 129], F32) for d in range(2)]
        self.Cdbf = [P.sb(t + "Cdbf_%d" % d, [128, 129], BF16) for d in range(2)]
        self.Cnew = [P.sb(t + "Cnew_%d" % d, [128, 129], F32) for d in range(2)]
        self.wqk = [P.sb(t + "wqk%d" % i, [128, 128], BF16) for i in range(4)]
        self.kw = [P.sb(t + "kw%d" % i, [128, 128], BF16) for i in range(4)]
        self.ep = [P.sb(t + "ep%d" % i, [128, 8], F32) for i in range(4)]
        self.hs = [P.sb(t + "hs%d" % i, [128, 128], F32) for i in range(2)]
        self.sqs = [P.sb(t + "sqs%d" % i, [128, 128], F32) for i in range(2)]
        self.ymix = [P.sb(t + "ymix%d" % i, [128, 128], BF16) for i in range(2)]
        self.cacc = [P.sb(t + "cacc%d" % i, [128, 512], F32) for i in range(2)]
        self.lam = P.sb(t + "lam", [128, 8], F32)
        self.ld = P.dma_sem(t + "ld")
        self.ldw = P.dma_sem(t + "ldw")
        self.ldbt = P.dma_sem(t + "ldbt")
        self.ldh = [P.dma_sem(t + "ldh%d" % i) for i in range(2)]
        self.st = P.dma_sem(t + "st")
        ar = self.arena
        self.Hf = ar[:, 0:8192].rearrange("p (c d) -> p c d", d=128)
        self.sigo = ar[:, 8192:12288].bitcast(BF16).rearrange("p (c d) -> p c d", d=128)
        self.ring = [[ar[:, 12288 + (cc * 3 + s) * 516: 12288 + (cc * 3 + s + 1) * 516] for s in range(3)]
                     for cc in range(2)]
        self.bt = ar[:, 0:3072].rearrange("p (i q) -> p i q", q=512)
        self.praw = [ar[:, 3072 + i * 512: 3072 + (i + 1) * 512] for i in range(2)]
        self.sq = [ar[:, 4096 + i * 512: 4096 + (i + 1) * 512] for i in range(2)]
        self.rstd = [ar[:, 5120 + i * 512: 5120 + (i + 1) * 512] for i in range(2)]
        self.sbias = [ar[:, 6144 + i * 512: 6144 + (i + 1) * 512] for i in range(4)]
        self.pT = [ar[:, 8192 + i * 256: 8192 + (i + 1) * 256].bitcast(BF16) for i in range(8)]
        self.osb = [ar[:, 10240 + i * 128: 10240 + (i + 1) * 128] for i in range(4)]

    def pv(self, name, *shape):
        o, w = PL.off[name]
        v = self.par[:, o:o + w]
        if len(shape) == 2:
            v = v.rearrange("p (a b) -> p a b", b=shape[1])
        elif len(shape) == 3:
            v = v.rearrange("p (a b c) -> p a b c", b=shape[1], c=shape[2])
        return v

    def bank(self, pool="gen"):
        lst = self.pools[pool]
        n = self.P.pool_ctr.get((self.tag, pool), 0)
        self.P.pool_ctr[(self.tag, pool)] = n + 1
        i = lst[n % len(lst)]
        key = self.k("bank", i)
        self.P.regroup(key)
        return self.banks[i], key

    def m_keys(self):
        K = self.k
        return ([K("Hf", c) for c in range(NCH)] + [K("sigo", c) for c in range(NCH)]
                + [K("ring", cc, s) for cc in range(2) for s in range(3)])

    def a_keys(self):
        K = self.k
        return ([K("bt")] + [K(nm, cc) for nm in ("praw", "sq", "rstd") for cc in range(2)]
                + [K("sbias", i) for i in range(4)] + [K("pT", i) for i in range(8)]
                + [K("osb", i) for i in range(4)])

    def fence(self, wkeys):
        self.P.op("pool", lambda e: e.memset(self.small[:, 8:9], 0.0), w=list(wkeys) + [self.k("epoch")])

    def run(self):
        P = self.P
        K = self.k
        P.op("sp", lambda e: e.dma_start(out=self.par[:], in_=self.par_d), w=[K("par")], dsem=self.ld)
        ident = self.pv("ident")
        P.op("dve", lambda e: e.tensor_copy(self.identb[:], ident), r=[K("par")], w=[K("identb")])
        P.op("pool", lambda e: e.memset(self.Vaug[:, :, 128:129], 1.0), w=[K("Vones")])
        for hd in self.heads:
            if hd < 2:
                self.mlstm_head(hd)
            else:
                self.attn_head(hd - 2)

    def load_w(self, col0, ncols):
        P, K = self.P, self.k
        src = self.wA.rearrange("(kc p) n -> p kc n", p=128)[:, :, col0:col0 + ncols]
        P.op("pool", lambda e: e.dma_start(out=self.Wbf[:, :, 0:ncols], in_=src), w=[K("Wbf")], dsem=self.ldw)

    def load_hT(self, j):
        P, K = self.P, self.k
        src = self.hT.rearrange("(kc p) t -> p kc t", p=128)[:, :, j * 512:(j + 1) * 512]
        b = j % 2
        P.op("sp", lambda e: e.dma_start(out=self.hTt[b][:], in_=src), w=[K("hTt", b)], dsem=self.ldh[b])

    def fm_proj(self, j, c0):
        P, K = self.P, self.k
        b = j % 2
        bk, bkk = self.bank()
        for kc in range(8):
            P.op("pe", lambda e, kc=kc: e.matmul(bk[:], self.Wbf[:, kc, c0:c0 + 128], self.hTt[b][:, kc, :],
                                                 start=(kc == 0), stop=(kc == 7)),
                 r=[K("Wbf"), K("hTt", b)], w=[bkk])
        return bk, bkk

    def tm_proj(self, j, sub, c0, nc_):
        P, K = self.P, self.k
        b = j % 2
        bk, bkk = self.bank()
        for kc in range(8):
            P.op("pe", lambda e, kc=kc: e.matmul(bk[:, 0:nc_], self.hTt[b][:, kc, sub * 128:(sub + 1) * 128],
                                                 self.Wbf[:, kc, c0:c0 + nc_], start=(kc == 0), stop=(kc == 7)),
                 r=[K("Wbf"), K("hTt", b)], w=[bkk])
        return bk, bkk

    def mlstm_head(self, j):
        P, K = self.P, self.k
        self.load_w(j * 516, 516)
        self.fence(self.a_keys())
        EP = K("epoch")
        cw = self.pv("cw", 2, 2, 5)
        cb = self.pv("cb", 2, 2)
        gb = self.pv("gb", 2, 4)
        ring = self.ring
        dst = [self.qbf, self.kbf]
        P.op("pool", lambda e: e.memset(self.arena[:, 12288:12288 + 6 * 516], 0.0),
             r=[EP], w=[K("ring", cc, s) for cc in range(2) for s in range(3)])

        def conv_tile(jt):
            s = jt % 3
            for cc in range(2):
                rg = ring[cc][s]
                acc = self.cacc[cc]
                P.op("dve", lambda e, rg=rg, acc=acc, cc=cc: e.tensor_scalar(
                    acc[:], rg[:, 0:512], cw[:, j, cc, 0:1], cb[:, j, cc:cc + 1], ALU.mult, ALU.add),
                    r=[K("ring", cc, s), K("par")], w=[K("cacc", cc)])
                for tp in range(1, 5):
                    P.op("dve", lambda e, rg=rg, acc=acc, cc=cc, tp=tp: e.scalar_tensor_tensor(
                        acc[:], rg[:, tp:tp + 512], cw[:, j, cc, tp:tp + 1], acc[:], ALU.mult, ALU.add),
                        r=[K("ring", cc, s), K("par"), K("cacc", cc)], w=[K("cacc", cc)])
                P.op("act", lambda e, acc=acc, cc=cc: e.activation(
                    out=dst[cc][:, jt * 512:(jt + 1) * 512], in_=acc[:], func=AF.Silu),
                    r=[K("cacc", cc)], w=[K("qk", cc, jt)])

        for jt in range(NT):
            self.load_hT(jt)
            s = jt % 3
            for cc in range(2):
                bk, bkk = self.fm_proj(jt, cc * 128)
                if jt == NT - 1:
                    P.op("pool", lambda e, cc=cc, s=s: e.memset(ring[cc][s][:, 514:516], 0.0),
                         w=[K("ring", cc, s)])
                P.op("act", lambda e, bk=bk, cc=cc, s=s: e.copy(ring[cc][s][:, 2:514], bk[:]),
                     r=[bkk], w=[K("ring", cc, s)])
                if jt > 0:
                    sp_ = (jt - 1) % 3
                    P.op("pool", lambda e, cc=cc, s=s, sp_=sp_: e.tensor_copy(
                        ring[cc][sp_][:, 514:516], ring[cc][s][:, 2:4]),
                        r=[K("ring", cc, s)], w=[K("ring", cc, sp_)])
                    P.op("pool", lambda e, cc=cc, s=s, sp_=sp_: e.tensor_copy(
                        ring[cc][s][:, 0:2], ring[cc][sp_][:, 512:514]),
                        r=[K("ring", cc, sp_)], w=[K("ring", cc, s)])
            for sub in range(4):
                c = jt * 4 + sub
                bk, bkk = self.tm_proj(jt, sub, 256, 260)
                P.op("dve", lambda e, bk=bk, c=c: e.tensor_copy(self.Vaug[:, c, 0:128], bk[:, 0:128]),
                     r=[bkk], w=[K("V", c)])
                P.op("act", lambda e, bk=bk, c=c: e.activation(out=self.sigo[:, c, :], in_=bk[:, 128:256],
                                                               func=AF.Sigmoid),
                     r=[bkk, EP], w=[K("sigo", c)])
                P.op("dve", lambda e, bk=bk, c=c: e.tensor_tensor(self.G[:, c, :], bk[:, 256:260], gb[:, j, :],
                                                                  ALU.add),
                     r=[bkk, K("par")], w=[K("G")])
            if jt > 0:
                conv_tile(jt - 1)
        conv_tile(NT - 1)
        self.mlstm_gates(j)
        self.mlstm_scan(j)
        P.op("sp", lambda e: e.dma_start(out=self.mixT_d[self.rows_m + j * 128:self.rows_m + (j + 1) * 128, :], in_=self.mixT[:]),
             r=[K("mixT", c) for c in range(NCH)], w=[K("mixTall")], dsem=self.st)

    def mlstm_gates(self, j):
        P, K = self.P, self.k
        ident = self.pv("ident")
        ones = self.pv("ones")
        R = self.Rt
        for gt in range(4):
            bk, bkk = self.bank()
            P.op("pe", lambda e, bk=bk, gt=gt: e.transpose(bk[0:64, 0:128], self.G[:, :, gt], ident),
                 r=[K("G"), K("par")], w=[bkk])
            P.op("dve", lambda e, bk=bk, gt=gt: e.tensor_copy(R[gt][:], bk[0:64, 0:128]), r=[bkk], w=[K("R", gt)])
        T = self.Rtmp
        lnc = -0.5 * math.log(128.0)
        for d in range(2):
            I, Fr = R[2 * d], R[2 * d + 1]
            e1, B, a, cm, zero = T[0], T[1], T[2], T[3], T[4]

            def rv(t_):
                if d == 0:
                    return t_[:]
                return bass.AP(t_, 127, [[128, 64], [-1, 128]])
            last = 127 if d == 0 else 0
            P.op("pool", lambda e: e.memset(zero[:], 0.0), w=[K("T", 4)])
            P.op("act", lambda e, Fr=Fr: e.activation(out=e1[:], in_=Fr[:], func=AF.Exp, scale=-1.0),
                 r=[K("R", 2 * d + 1)], w=[K("T", 0)])
            P.op("act", lambda e: e.activation(out=e1[:], in_=e1[:], func=AF.Ln, bias=1.0),
                 r=[K("T", 0)], w=[K("T", 0)])
            P.op("dve", lambda e: e.tensor_tensor_scan(rv(B), rv(e1), rv(zero), 0.0, ALU.add, ALU.add),
                 r=[K("T", 0), K("T", 4)], w=[K("T", 1)])
            P.op("dve", lambda e, I=I: e.tensor_tensor(a[:], I[:], B[:], ALU.add),
                 r=[K("R", 2 * d), K("T", 1)], w=[K("T", 2)])
            P.op("dve", lambda e: e.tensor_tensor_scan(rv(cm), rv(a), rv(a), -1e30, ALU.max, ALU.max),
                 r=[K("T", 2)], w=[K("T", 3)])
            rc = self.Rcol
            rows = self.rows
            for q_, src in enumerate((cm, B)):
                bk, bkk = self.bank()
                P.op("pe", lambda e, bk=bk, src=src: e.transpose(bk[0:1, 0:64], src[:, last:last + 1],
                                                                 ident[0:64, 0:64]),
                     r=[K("T", 3), K("T", 1), K("par")], w=[bkk])
                P.op("dve", lambda e, bk=bk, q_=q_: e.tensor_copy(rows[0:1, q_ * 64:(q_ + 1) * 64], bk[0:1, 0:64]),
                     r=[bkk], w=[K("rows", q_)])
            def rrow(q_):
                if d == 0:
                    return rows[0:1, q_ * 64:(q_ + 1) * 64]
                return bass.AP(self.rows, q_ * 64 + 63, [[512, 1], [-1, 64]])
            P.op("dve", lambda e: e.tensor_tensor_scan(rrow(2), rrow(0), rrow(1), 0.0, ALU.max, ALU.subtract),
                 r=[K("rows", 0), K("rows", 1)], w=[K("rows", 2)])
            P.op("pool", lambda e: e.memset(rows[0:1, 192:256], 0.0), w=[K("rows", 3)])
            if d == 0:
                P.op("dve", lambda e: e.tensor_copy(rows[0:1, 193:256], rows[0:1, 128:191]),
                     r=[K("rows", 2)], w=[K("rows", 3)])
            else:
                P.op("dve", lambda e: e.tensor_copy(rows[0:1, 192:255], rows[0:1, 129:192]),
                     r=[K("rows", 2)], w=[K("rows", 3)])
            bk, bkk = self.bank()
            P.op("pe", lambda e, bk=bk: e.matmul(bk[0:64, 0:1], rows[0:1, 192:256], ones[0:1, 0:1],
                                                 start=True, stop=True),
                 r=[K("rows", 3), K("par")], w=[bkk])
            m_in, negML, dec = rc[:, 0:1], rc[:, 1:2], rc[:, 2:3]
            negMLc = rc[:, 3:4]
            P.op("dve", lambda e, bk=bk: e.tensor_copy(m_in, bk[0:64, 0:1]), r=[bkk], w=[K("rc", 0)])
            P.op("dve", lambda e: e.tensor_scalar(negML, cm[:, last:last + 1], m_in, -1.0, ALU.max, ALU.mult),
                 r=[K("T", 3), K("rc", 0)], w=[K("rc", 1)])
            P.op("dve", lambda e: e.tensor_scalar(negMLc, negML, lnc, None, ALU.add),
                 r=[K("rc", 1)], w=[K("rc", 3)])
            ws_r, rr_r = T[5], T[0]
            P.op("act", lambda e: e.activation(out=ws_r[:], in_=a[:], func=AF.Exp, bias=negMLc),
                 r=[K("T", 2), K("rc", 3)], w=[K("T", 5)])
            P.op("act", lambda e: e.activation(out=rr_r[:], in_=B[:], func=AF.Exp, bias=negML),
                 r=[K("T", 1), K("rc", 1), K("T", 0)], w=[K("T", 0)])
            P.op("act", lambda e: e.activation(out=dec, in_=m_in, func=AF.Exp, bias=negML),
                 r=[K("rc", 0), K("rc", 1)], w=[K("rc", 2)])
            for src, dstt, kk in ((ws_r, self.WS[d], "WS"), (rr_r, self.RR[d], "RR")):
                bk, bkk = self.bank()
                P.op("pe", lambda e, bk=bk, src=src: e.transpose(bk[:, 0:64], src[:], ident[0:64, 0:64]),
                     r=[K("T", 5), K("T", 0), K("par")], w=[bkk])
                P.op("dve", lambda e, bk=bk, dstt=dstt: e.tensor_copy(dstt[:], bk[:, 0:64]),
                     r=[bkk], w=[K(kk, d)])
            P.op("dve", lambda e: e.tensor_scalar(self.diag[:], ident[0:64, 0:64], dec, None, ALU.mult),
                 r=[K("rc", 2), K("par")], w=[K("diag")])
            bk, bkk = self.bank()
            P.op("pe", lambda e, bk=bk: e.matmul(bk[:, 0:64], ones[0:64, :], self.diag[:], start=True, stop=True),
                 r=[K("diag"), K("par")], w=[bkk])
            P.op("dve", lambda e, bk=bk: e.tensor_copy(self.DEC[d][:, 0:64], bk[:, 0:64]), r=[bkk], w=[K("DEC", d)])

    def mlstm_scan(self, j):
        P, K = self.P, self.k
        maskv = [self.pv("maskf"), self.pv("maskb")]
        EP = K("epoch")
        ng = self.pv("ng", 2, 128)
        for d in range(2):
            P.op("pool", lambda e, d=d: e.memset(self.Cd32[d][:], 0.0), w=[K("Cd32", d)])
            P.op("pool", lambda e, d=d: e.memset(self.Cdbf[d][:], 0.0), w=[K("Cdbf", d)])
        AB = {}
        BB = {}

        def chunk_of(step, d):
            return step if d == 0 else NCH - 1 - step

        def stageAB(step, d):
            c = chunk_of(step, d)
            cs = slice(c * 128, (c + 1) * 128)
            jt = c // 4
            idx = (step * 2 + d) % 4
            bk, bkk = self.bank()
            st_ps = bk[:, 0:128]
            kt_ps = bk[:, 128:192].bitcast(BF16)
            P.op("pe", lambda e: e.matmul(st_ps, self.kbf[:, cs], self.qbf[:, cs], start=True, stop=True),
                 r=[K("qk", 0, jt), K("qk", 1, jt)], w=[bkk])
            P.op("pe", lambda e: e.transpose(kt_ps, self.kbf[:, cs], self.identb[:]),
                 r=[K("qk", 1, jt), K("identb")], w=[bkk])
            wq = self.wqk[idx]
            kwt = self.kw[idx]
            P.op("dve", lambda e: e.scalar_tensor_tensor(wq[:], st_ps, self.WS[d][:, c:c + 1], maskv[d],
                                                         ALU.mult, ALU.mult),
                 r=[bkk, K("WS", d), K("par")], w=[K("wqk", idx)])
            P.op("act", lambda e: e.activation(out=kwt[:], in_=kt_ps, func=AF.Copy, scale=self.WS[d][:, c:c + 1]),
                 r=[bkk, K("WS", d)], w=[K("kw", idx)])

        def stageC(step, d):
            c = chunk_of(step, d)
            cs = slice(c * 128, (c + 1) * 128)
            jt = c // 4
            idx = (step * 2 + d) % 4
            bkB, bkkB = self.bank()
            BB[(step, d)] = (bkB, bkkB)
            out_ps = bkB[:, 0:129]
            kv_ps = bkB[:, 129:258]
            P.op("pe", lambda e: e.matmul(out_ps, self.wqk[idx][:], self.Vaug[:, c, :], start=True, stop=False),
                 r=[K("wqk", idx), K("V", c), K("Vones")], w=[bkkB])
            P.op("pe", lambda e: e.matmul(out_ps, self.qbf[:, cs], self.Cdbf[d][:], start=False, stop=True),
                 r=[K("qk", 0, jt), K("Cdbf", d)], w=[bkkB])
            P.op("pe", lambda e: e.matmul(kv_ps, self.kw[idx][:], self.Vaug[:, c, :], start=True, stop=True),
                 r=[K("kw", idx), K("V", c), K("Vones")], w=[bkkB])

        def stageD(step, d):
            if step >= NCH - 1:
                return
            c = chunk_of(step, d)
            bkB, bkkB = BB[(step, d)]
            kv_ps = bkB[:, 129:258]
            cn = c + 1 if d == 0 else c - 1
            P.op("dve", lambda e: e.tensor_tensor(self.Cnew[d][:], kv_ps, self.Cd32[d][:], ALU.add),
                 r=[bkkB, K("Cd32", d)], w=[K("Cnew", d)])
            P.op("act", lambda e: e.activation(out=self.Cdbf[d][:], in_=self.Cnew[d][:], func=AF.Copy,
                                               scale=self.DEC[d][:, cn:cn + 1]),
                 r=[K("Cnew", d), K("DEC", d)], w=[K("Cdbf", d)])
            P.op("pool", lambda e: e.tensor_scalar(self.Cd32[d][:], self.Cnew[d][:], self.DEC[d][:, cn:cn + 1],
                                                   None, ALU.mult),
                 r=[K("Cnew", d), K("DEC", d)], w=[K("Cd32", d)])

        def stageE(step, d):
            c = chunk_of(step, d)
            cs = slice(c * 128, (c + 1) * 128)
            idx = (step * 2 + d) % 4
            i2 = (step * 2 + d) % 2
            bkB, bkkB = BB.pop((step, d))
            out_ps = bkB[:, 0:129]
            ep = self.ep[idx]
            P.op("dve", lambda e: e.tensor_scalar(ep[:, 0:1], out_ps[:, 128:129], -1.0, self.RR[d][:, c:c + 1],
                                                  ALU.mult, ALU.max),
                 r=[bkkB, K("RR", d)], w=[K("ep", idx)])
            P.op("dve", lambda e: e.tensor_tensor(ep[:, 0:1], ep[:, 0:1], out_ps[:, 128:129], ALU.max),
                 r=[bkkB, K("ep", idx)], w=[K("ep", idx)])
            P.op("dve", lambda e: e.reciprocal(ep[:, 1:2], ep[:, 0:1]), r=[K("ep", idx)], w=[K("ep", idx)])
            if step < NCH // 2:
                P.op("act", lambda e: e.activation(out=self.Hf[:, c, :], in_=out_ps[:, 0:128], func=AF.Copy,
                                                   scale=ep[:, 1:2]),
                     r=[bkkB, K("ep", idx), EP], w=[K("Hf", c)])
                return
            hs = self.hs[i2]
            sq = self.sqs[i2]
            ym = self.ymix[i2]
            P.op("dve", lambda e: e.scalar_tensor_tensor(hs[:], out_ps[:, 0:128], ep[:, 1:2], self.Hf[:, c, :],
                                                         ALU.mult, ALU.add),
                 r=[bkkB, K("ep", idx), K("Hf", c)], w=[K("hs", i2)])
            P.op("pool", lambda e: e.memset(ep[:, 2:3], 0.0), w=[K("ep", idx)])
            P.op("act", lambda e: e.activation(out=sq[:], in_=hs[:], func=AF.Square, accum_out=ep[:, 2:3]),
                 r=[K("hs", i2)], w=[K("sqs", i2), K("ep", idx)])
            P.op("act", lambda e: e.activation(out=ep[:, 3:4], in_=ep[:, 2:3], func=AF.Sqrt, scale=1.0 / 128.0,
                                               bias=self.small[:, 0:1]),
                 r=[K("ep", idx), K("eps")], w=[K("ep", idx)])
            P.op("dve", lambda e: e.reciprocal(ep[:, 4:5], ep[:, 3:4]), r=[K("ep", idx)], w=[K("ep", idx)])
            P.op("dve", lambda e: e.scalar_tensor_tensor(hs[:], hs[:], ep[:, 4:5], ng[:, j, :], ALU.mult, ALU.mult),
                 r=[K("hs", i2), K("ep", idx), K("par")], w=[K("hs", i2)])
            P.op("pool", lambda e: e.tensor_tensor(ym[:], hs[:], self.sigo[:, c, :], ALU.mult),
                 r=[K("hs", i2), K("sigo", c)], w=[K("ymix", i2)])
            bk2, bkk2 = self.bank()
            tp = bk2[:, 0:64].bitcast(BF16)
            P.op("pe", lambda e: e.transpose(tp, ym[:], self.identb[:]), r=[K("ymix", i2), K("identb")], w=[bkk2])
            P.op("act", lambda e: e.copy(self.mixT[:, cs], tp), r=[bkk2, K("mixTall")], w=[K("mixT", c)])

        for d in range(2):
            stageAB(0, d)
        for step in range(NCH):
            if step + 1 < NCH:
                for d in range(2):
                    stageAB(step + 1, d)
            for d in range(2):
                stageC(step, d)
            for d in range(2):
                stageD(step, d)
            for d in range(2):
                stageE(step, d)

    def attn_head(self, j):
        P, K = self.P, self.k
        self.load_w(1032 + j * 384, 384)
        self.fence(self.m_keys())
        EP = K("epoch")
        P.op("sp", lambda e: e.dma_start(out=self.bt, in_=self.bt_d[:, j, :, :]),
             r=[EP], w=[K("bt")], dsem=self.ldbt)
        blk = self.pv("blk")
        gvec = [self.pv("gq"), self.pv("gk")]
        dstb = [self.qbf, self.kbf]
        rb = self.pv("rb", 2, 2)
        dg = self.pv("dg", 2, 128)
        lv = self.pv("lv", 4, 64)
        li = self.pv("li")
        lam = self.lam
        for q_ in range(2):
            P.op("dve", lambda e, q_=q_: e.tensor_tensor(self.sqs[0][:, 0:64], lv[:, 2 * q_, :], lv[:, 2 * q_ + 1, :],
                                                        ALU.mult),
                 r=[K("par"), K("sqs", 0)], w=[K("sqs", 0)])
            P.op("dve", lambda e, q_=q_: e.tensor_reduce(lam[:, q_:q_ + 1], self.sqs[0][:, 0:64],
                                                        mybir.AxisListType.X, ALU.add),
                 r=[K("sqs", 0)], w=[K("lam")])
        P.op("act", lambda e: e.activation(out=lam[:, 2:4], in_=lam[:, 0:2], func=AF.Exp), r=[K("lam")], w=[K("lam")])
        P.op("dve", lambda e: e.tensor_tensor(lam[:, 4:5], lam[:, 3:4], lam[:, 2:3], ALU.subtract),
             r=[K("lam")], w=[K("lam")])
        P.op("dve", lambda e: e.tensor_tensor(lam[:, 5:6], lam[:, 4:5], li[:, 0:1], ALU.subtract),
             r=[K("lam"), K("par")], w=[K("lam")])
        for jt in range(NT):
            self.load_hT(jt)
            for cc in range(2):
                bk, bkk = self.fm_proj(jt, cc * 128)
                praw, sq, rstd = self.praw[cc], self.sq[cc], self.rstd[cc]
                P.op("act", lambda e, bk=bk, sq=sq: e.activation(out=sq, in_=bk[:], func=AF.Square),
                     r=[bkk, EP], w=[K("sq", cc)])
                P.op("dve", lambda e, bk=bk, praw=praw: e.tensor_copy(praw, bk[:]), r=[bkk, EP], w=[K("praw", cc)])
                bk2, bkk2 = self.bank()
                P.op("pe", lambda e, bk2=bk2, sq=sq: e.matmul(bk2[:], blk, sq, start=True, stop=True),
                     r=[K("sq", cc), K("par")], w=[bkk2])
                P.op("act", lambda e, bk2=bk2, rstd=rstd: e.activation(out=rstd, in_=bk2[:], func=AF.Sqrt,
                                                                     scale=1.0 / 64.0, bias=self.small[:, 0:1]),
                     r=[bkk2, K("eps"), EP], w=[K("rstd", cc)])
                P.op("dve", lambda e, rstd=rstd: e.reciprocal(rstd, rstd), r=[K("rstd", cc)], w=[K("rstd", cc)])
                P.op("dve", lambda e, praw=praw, rstd=rstd, cc=cc, jt=jt: e.scalar_tensor_tensor(
                    dstb[cc][:, jt * 512:(jt + 1) * 512], praw, gvec[cc], rstd, ALU.mult, ALU.mult),
                    r=[K("praw", cc), K("rstd", cc), K("par")], w=[K("qk", cc, jt)])
            for sub in range(4):
                c = jt * 4 + sub
                bk, bkk = self.tm_proj(jt, sub, 256, 128)
                P.op("act", lambda e, bk=bk, c=c: e.copy(self.Vaug[:, c, 0:128], bk[:, 0:128]),
                     r=[bkk], w=[K("V", c)])
        LOOK = 3
        units = [(qt, kb, m) for qt in range(NT) for kb in range(NCH) for m in range(2)]
        sc = {}

        def emit_qk(ui):
            qt, kb, m = units[ui]
            qs = slice(qt * 512, (qt + 1) * 512)
            ps_ = slice(m * 64, (m + 1) * 64)
            bk, bkk = self.bank("sc")
            P.op("pe", lambda e: e.matmul(bk[:], self.kbf[ps_, kb * 128:(kb + 1) * 128], self.qbf[ps_, qs],
                                          start=True, stop=True),
                 r=[K("qk", 0, qt), K("qk", 1, kb // 4)], w=[bkk])
            sc[ui] = (bk, bkk)

        def epilogue(qt, acc):
                for sub in range(4):
                    a1, ak1 = acc[sub]
                    a2, ak2 = acc[4 + sub]
                    ep = self.ep[sub]
                    osb = self.osb[sub]
                    ym = self.ymix[sub % 2]
                    c = qt * 4 + sub
                    P.op("dve", lambda e, a1=a1, ep=ep: e.reciprocal(ep[:, 0:1], a1[:, 128:129]),
                         r=[ak1], w=[K("ep", sub)])
                    P.op("dve", lambda e, a2=a2, ep=ep: e.reciprocal(ep[:, 1:2], a2[:, 128:129]),
                         r=[ak2], w=[K("ep", sub)])
                    P.op("dve", lambda e, ep=ep: e.tensor_tensor(ep[:, 1:2], ep[:, 1:2], lam[:, 5:6], ALU.mult),
                         r=[K("ep", sub), K("lam")], w=[K("ep", sub)])
                    P.op("act", lambda e, a1=a1, ep=ep, osb=osb: e.activation(out=osb, in_=a1[:, 0:128], func=AF.Copy,
                                                                             scale=ep[:, 0:1]),
                         r=[ak1, K("ep", sub), EP], w=[K("osb", sub)])
                    P.op("dve", lambda e, a2=a2, ep=ep, osb=osb: e.scalar_tensor_tensor(
                        osb, a2[:, 0:128], ep[:, 1:2], osb, ALU.mult, ALU.add),
                        r=[ak2, K("ep", sub), K("osb", sub)], w=[K("osb", sub)])
                    sq = self.sqs[sub % 2]
                    P.op("pool", lambda e, ep=ep: e.memset(ep[:, 2:3], 0.0), w=[K("ep", sub)])
                    P.op("act", lambda e, osb=osb, sq=sq, ep=ep: e.activation(out=sq[:], in_=osb, func=AF.Square,
                                                                             accum_out=ep[:, 2:3]),
                         r=[K("osb", sub)], w=[K("sqs", sub % 2), K("ep", sub)])
                    P.op("act", lambda e, ep=ep: e.activation(out=ep[:, 3:4], in_=ep[:, 2:3], func=AF.Sqrt,
                                                              scale=1.0 / 128.0, bias=self.small[:, 0:1]),
                         r=[K("ep", sub), K("eps")], w=[K("ep", sub)])
                    P.op("dve", lambda e, ep=ep: e.reciprocal(ep[:, 4:5], ep[:, 3:4]), r=[K("ep", sub)], w=[K("ep", sub)])
                    P.op("dve", lambda e, ep=ep: e.tensor_tensor(ep[:, 4:5], ep[:, 4:5], li[:, 1:2], ALU.mult),
                         r=[K("ep", sub), K("par")], w=[K("ep", sub)])
                    P.op("dve", lambda e, osb=osb, ep=ep, ym=ym: e.scalar_tensor_tensor(
                        ym[:], osb, ep[:, 4:5], dg[:, j, :], ALU.mult, ALU.mult),
                        r=[K("osb", sub), K("ep", sub), K("par")], w=[K("ymix", sub % 2)])
                    bk2, bkk2 = self.bank("tp")
                    tp = bk2[:, 0:64].bitcast(BF16)
                    P.op("pe", lambda e, tp=tp, ym=ym: e.transpose(tp, ym[:], self.identb[:]),
                         r=[K("ymix", sub % 2), K("identb")], w=[bkk2])
                    P.op("act", lambda e, tp=tp, c=c: e.copy(self.mixT[:, c * 128:(c + 1) * 128], tp),
                         r=[bkk2, K("mixTall")], w=[K("mixT", c)])

        issued = 0
        acc = None
        for n, (qt, kb, m) in enumerate(units):
            if n % 2 == 0:
                while issued < min(n + 4, len(units)):
                    emit_qk(issued)
                    issued += 1
            if kb == 0 and m == 0:
                accb = [self.bank("acc") for _ in range(3)]
                acc = []
                for i in range(8):
                    b_, bk_ = accb[i // 3]
                    o = (i % 3) * 129
                    acc.append((b_[:, o:o + 129], bk_))
            bk, bkk = sc.pop(n)
            i_near = kb - 4 * qt + 1
            near = 0 <= i_near < 6
            pt = self.pT[n % 8]
            if near:
                sb_ = self.sbias[n % 4]
                P.op("dve", lambda e: e.scalar_tensor_tensor(sb_, bk[:], 0.125, self.bt[:, i_near, :],
                                                             ALU.mult, ALU.add),
                     r=[bkk, K("bt"), EP], w=[K("sbias", n % 4)])
                P.op("act", lambda e: e.activation(out=pt, in_=sb_, func=AF.Exp),
                     r=[K("sbias", n % 4), EP], w=[K("pT", n % 8)])
            else:
                side = 0 if kb < 4 * qt else 1
                P.op("act", lambda e: e.activation(out=pt, in_=bk[:], func=AF.Exp, scale=0.125,
                                                   bias=rb[:, j, side:side + 1]),
                     r=[bkk, K("par"), EP], w=[K("pT", n % 8)])
            for sub in range(4):
                ai = m * 4 + sub
                a_, ak = acc[ai]
                P.op("pe", lambda e: e.matmul(a_, pt[:, sub * 128:(sub + 1) * 128], self.Vaug[:, kb, :],
                                              start=(kb == 0 and ai % 3 == 0),
                                              stop=(kb == NCH - 1 and ai in (2, 5, 7))),
                     r=[K("pT", n % 8), K("V", kb), K("Vones")], w=[ak])
            if kb == NCH - 1 and m == 1:
                epilogue(qt, acc)
        P.op("sp", lambda e: e.dma_start(out=self.mixT_d[self.rows_a + j * 128:self.rows_a + (j + 1) * 128, :], in_=self.mixT[:]),
             r=[K("mixT", c) for c in range(NCH)], w=[K("mixTall")], dsem=self.st)


def init_small(P, small, key):
    P.op("pool", lambda e: e.memset(small[:, 0:1], EPS), w=[key])


def build_A(heads=(0, 1, 2, 3)):
    nc = bass.Bass("TRN2", target_bir_lowering=False)
    P = Prog(nc)
    hT = nc.dram_tensor("hT", [D_MODEL, SEQ], BF16, kind="ExternalInput").ap()
    wA = nc.dram_tensor("wA", [D_MODEL, 1800], F32, kind="ExternalInput").ap()
    par_d = nc.dram_tensor("par", [128, PL.n], F32, kind="ExternalInput").ap()
    bt_d = nc.dram_tensor("bt", [128, 2, 6, 512], F32, kind="ExternalInput").ap()
    mixT_d = nc.dram_tensor("mixT", [512, SEQ], BF16, kind="ExternalOutput").ap()
    A = PhaseA(P, hT, wA, par_d, bt_d, mixT_d, heads=heads)
    init_small(P, A.small, A.k("eps"))
    A.run()
    P.emit()
    return nc


TOK = HALF
NTB = TOK // 512


class PhaseB:
    def __init__(self, P, xT_in, mixT_all, w_out, w_ff1, w_ff2, gpar, xT_out, hT_next, w1b, w2b,
                 mode="full", tag="B"):
        self.P, self.tag, self.mode = P, tag, mode
        self.xT_in, self.mixT_all, self.w_out, self.w_ff1, self.w_ff2 = xT_in, mixT_all, w_out, w_ff1, w_ff2
        self.gpar, self.xT_out, self.hT_next, self.w1b, self.w2b = gpar, xT_out, hT_next, w1b, w2b
        self.alloc()

    def k(self, *a):
        return (self.tag,) + a

    def alloc(self):
        P, t = self.P, self.tag
        full = self.mode == "full"
        self.gp = P.sb(t + "gp", [128, 16 + 128], F32)
        self.small = P.sb(t + "small", [128, 16], F32)
        self.xT = [P.sb(t + "xT%d" % i, [128, 8, 512], F32) for i in range(2)]
        self.hTn = P.sb(t + "hTn", [128, 8, 512], BF16)
        self.sq = [P.sb(t + "sq%d" % i, [128, 512], F32) for i in range(2)]
        self.rstd = P.sb(t + "rstd", [128, 512], F32)
        self.banks = [P.ps(t + "bank%d" % i, [128, 512], F32) for i in range(8)]
        self.ldx = [P.dma_sem(t + "ldx%d" % i) for i in range(2)]
        self.ldg = P.dma_sem(t + "ldg")
        self.st = P.dma_sem(t + "st")
        self.stx = [P.dma_sem(t + "stx%d" % i) for i in range(2)]
        if full:
            self.mixt = P.sb(t + "mixt", [128, 8, 512], BF16)
            self.h2T = P.sb(t + "h2T", [128, 8, 512], BF16)
            self.uT = P.sb(t + "uT", [128, 32, 512], BF16)
            self.Woutb = P.sb(t + "Woutb", [128, 8, 1024], BF16)
            self.W1t = [P.sb(t + "W1t%d" % i, [128, 8, 512], BF16) for i in range(2)]
            self.W2t = [P.sb(t + "W2t%d" % i, [128, 32, 256], BF16) for i in range(2)]
            self.rl = [P.sb(t + "rl%d" % i, [128, 512], F32) for i in range(2)]
            self.ldm = P.dma_sem(t + "ldm")
            self.ldw1 = [P.dma_sem(t + "ldw1_%d" % i) for i in range(2)]
            self.ldw2 = [P.dma_sem(t + "ldw2_%d" % i) for i in range(2)]
            self.cst = P.dma_sem(t + "cst")
            self.ldwo = P.dma_sem(t + "ldwo")
        self.nb = 0

    def bank(self):
        i = self.nb % 8
        self.nb += 1
        key = self.k("bank", i)
        self.P.regroup(key)
        return self.banks[i], key

    def norm(self, xt, xkeys, gcol0, dst, dkey):
        P, K = self.P, self.k
        ones = self.gp[:, 16:144]
        bk, bkk = self.bank()
        for kc in range(8):
            sq = self.sq[kc % 2]
            P.op("act", lambda e: e.activation(out=sq[:], in_=xt[:, kc, :], func=AF.Square),
                 r=[xkeys[kc]], w=[K("sq", kc % 2)])
            P.op("pe", lambda e: e.matmul(bk[:], ones, sq[:], start=(kc == 0), stop=(kc == 7)),
                 r=[K("sq", kc % 2), K("gp")], w=[bkk])
        P.op("act", lambda e: e.activation(out=self.rstd[:], in_=bk[:], func=AF.Sqrt, scale=1.0 / D_MODEL,
                                           bias=self.small[:, 0:1]),
             r=[bkk, K("eps")], w=[K("rstd")])
        P.op("dve", lambda e: e.reciprocal(self.rstd[:], self.rstd[:]), r=[K("rstd")], w=[K("rstd")])
        for kc in range(8):
            P.op("dve", lambda e: e.scalar_tensor_tensor(dst[:, kc, :], xt[:, kc, :],
                                                         self.gp[:, gcol0 + kc:gcol0 + kc + 1], self.rstd[:],
                                                         ALU.mult, ALU.mult),
                 r=[xkeys[kc], K("rstd"), K("gp")], w=[dkey])

    def run(self):
        P, K = self.P, self.k
        full = self.mode == "full"
        P.op("pool", lambda e: e.memset(self.small[:, 0:1], EPS), w=[K("eps")])
        P.op("sp", lambda e: e.dma_start(out=self.gp[:], in_=self.gpar), w=[K("gp")], dsem=self.ldg)
        if full:
            src = self.w_out.rearrange("(kc p) n -> p kc n", p=128)
            P.op("pool", lambda e: e.dma_start(out=self.Woutb[:], in_=src), w=[K("Woutb")], dsem=self.ldwo)
            for i in range(4):
                s1 = self.w_ff1.rearrange("(a p) n -> p a n", p=128)[:, 2 * i:2 * i + 2, :]
                d1 = self.w1b.rearrange("(a p) n -> p a n", p=128)[:, 2 * i:2 * i + 2, :]
                P.op("pool", lambda e: e.dma_start(out=d1, in_=s1), w=[K("w1b"), K("castchain")], dsem=self.cst)
            for i in range(4):
                s2 = self.w_ff2.rearrange("(a p) n -> p a n", p=128)[:, 8 * i:8 * i + 8, :]
                d2 = self.w2b.rearrange("(a p) n -> p a n", p=128)[:, 8 * i:8 * i + 8, :]
                P.op("pool", lambda e: e.dma_start(out=d2, in_=s2), w=[K("w2b"), K("castchain")], dsem=self.cst)
        nw1 = 0
        nw2 = 0
        for it in range(NTB):
            ts = slice(it * 512, (it + 1) * 512)
            xb = it % 2
            xt = self.xT[xb]
            xk = [K("xT", xb, kc) for kc in range(8)]
            src = self.xT_in.rearrange("(kc p) t -> p kc t", p=128)[:, :, ts]
            P.op("sp", lambda e: e.dma_start(out=xt[:], in_=src), w=xk, dsem=self.ldx[xb])
            if full:
                srcm = self.mixT_all.rearrange("(kc p) t -> p kc t", p=128)[:, :, ts]
                P.op("sp", lambda e: e.dma_start(out=self.mixt[:], in_=srcm), w=[K("mixt")], dsem=self.ldm)
                for dmc in range(8):
                    bk, bkk = self.bank()
                    for kc in range(8):
                        P.op("pe", lambda e: e.matmul(bk[:], self.Woutb[:, kc, dmc * 128:(dmc + 1) * 128],
                                                      self.mixt[:, kc, :], start=(kc == 0), stop=(kc == 7)),
                             r=[K("Woutb"), K("mixt")], w=[bkk])
                    P.op("dve", lambda e: e.tensor_tensor(xt[:, dmc, :], xt[:, dmc, :], bk[:], ALU.add),
                         r=[bkk, xk[dmc]], w=[xk[dmc]])
                self.norm(xt, xk, 0, self.h2T, K("h2T"))
                for fg in range(8):
                    wb = nw1 % 2
                    nw1 += 1
                    s1 = self.w1b.rearrange("(kc p) n -> p kc n", p=128)[:, :, fg * 512:(fg + 1) * 512]
                    P.op("sp", lambda e: e.dma_start(out=self.W1t[wb][:], in_=s1), r=[K("w1b")], w=[K("W1t", wb)],
                         dsem=self.ldw1[wb])
                    for jj in range(4):
                        ffc = fg * 4 + jj
                        bk, bkk = self.bank()
                        for kc in range(8):
                            P.op("pe", lambda e: e.matmul(bk[:], self.W1t[wb][:, kc, jj * 128:(jj + 1) * 128],
                                                          self.h2T[:, kc, :], start=(kc == 0), stop=(kc == 7)),
                                 r=[K("W1t", wb), K("h2T")], w=[bkk])
                        rl = self.rl[ffc % 2]
                        P.op("act", lambda e: e.activation(out=rl[:], in_=bk[:], func=AF.Relu),
                             r=[bkk], w=[K("rl", ffc % 2)])
                        P.op("dve", lambda e: e.tensor_tensor(self.uT[:, ffc, :], rl[:], bk[:], ALU.mult),
                             r=[bkk, K("rl", ffc % 2)], w=[K("uT", ffc)])
                for dg in range(4):
                    wb = nw2 % 2
                    nw2 += 1
                    s2 = self.w2b.rearrange("(fc p) n -> p fc n", p=128)[:, :, dg * 256:(dg + 1) * 256]
                    P.op("sp", lambda e: e.dma_start(out=self.W2t[wb][:], in_=s2), r=[K("w2b")], w=[K("W2t", wb)],
                         dsem=self.ldw2[wb])
                    for dd in range(2):
                        dmc = dg * 2 + dd
                        bk, bkk = self.bank()
                        for ffc in range(32):
                            P.op("pe", lambda e: e.matmul(bk[:], self.W2t[wb][:, ffc, dd * 128:(dd + 1) * 128],
                                                          self.uT[:, ffc, :], start=(ffc == 0), stop=(ffc == 31)),
                                 r=[K("W2t", wb), K("uT", ffc)], w=[bkk])
                        P.op("dve", lambda e: e.tensor_tensor(xt[:, dmc, :], xt[:, dmc, :], bk[:], ALU.add),
                             r=[bkk, xk[dmc]], w=[xk[dmc]])
                dstx = self.xT_out.rearrange("(kc p) t -> p kc t", p=128)[:, :, ts]
                P.op("sp", lambda e: e.dma_start(out=dstx, in_=xt[:]), r=xk, dsem=self.stx[xb])
            self.norm(xt, xk, 8, self.hTn, K("hTn"))
            dsth = self.hT_next.rearrange("(kc p) t -> p kc t", p=128)[:, :, ts]
            P.op("sp", lambda e: e.dma_start(out=dsth, in_=self.hTn[:]), r=[K("hTn")], dsem=self.st)


def build_B(mode="full"):
    nc = bass.Bass("TRN2", target_bir_lowering=False)
    P = Prog(nc)
    xT_in = nc.dram_tensor("xT_in", [D_MODEL, TOK], F32, kind="ExternalInput").ap()
    gpar = nc.dram_tensor("gpar", [128, 144], F32, kind="ExternalInput").ap()
    hT_next = nc.dram_tensor("hT_next", [D_MODEL, TOK], BF16, kind="ExternalOutput").ap()
    if mode == "full":
        mixT_all = nc.dram_tensor("mixT_all", [D_MODEL, TOK], BF16, kind="ExternalInput").ap()
        w_out = nc.dram_tensor("w_out", [D_MODEL, D_MODEL], F32, kind="ExternalInput").ap()
        w_ff1 = nc.dram_tensor("w_ff1", [D_MODEL, D_FF], F32, kind="ExternalInput").ap()
        w_ff2 = nc.dram_tensor("w_ff2", [D_FF, D_MODEL], F32, kind="ExternalInput").ap()
        xT_out = nc.dram_tensor("xT_out", [D_MODEL, TOK], F32, kind="ExternalOutput").ap()
        w1b = nc.dram_tensor("w1b", [D_MODEL, D_FF], BF16).ap()
        w2b = nc.dram_tensor("w2b", [D_FF, D_MODEL], BF16).ap()
    else:
        mixT_all = w_out = w_ff1 = w_ff2 = xT_out = w1b = w2b = None
    B = PhaseB(P, xT_in, mixT_all, w_out, w_ff1, w_ff2, gpar, xT_out, hT_next, w1b, w2b, mode=mode)
    B.run()
    P.emit()
    return nc


def build_gpar(g_a, g_b):
    gp = np.zeros((128, 144), np.float32)
    gp[:, 0:8] = np.asarray(g_a, np.float32).reshape(8, 128).T
    gp[:, 8:16] = np.asarray(g_b, np.float32).reshape(8, 128).T
    gp[:, 16:144] = 1.0
    return gp


GPW = 16 + 128 + 128


def build_gpar2(g_a, g_b):
    gp = np.zeros((128, GPW), np.float32)
    gp[:, 0:8] = np.asarray(g_a, np.float32).reshape(8, 128).T
    gp[:, 8:16] = np.asarray(g_b, np.float32).reshape(8, 128).T
    gp[:, 16:144] = 1.0
    gp[:, 144:272] = np.eye(128, dtype=np.float32)
    return gp


class PhaseB2:
    def __init__(self, P, carve, banks, ntiles, tag="B"):
        self.P, self.tag, self.nt = P, tag, ntiles
        self.banks = banks
        t = tag
        self.gp = P.sb(t + "gp", [128, GPW], F32)
        self.small = P.sb(t + "small", [128, 16], F32)
        self.sq = [P.sb(t + "sq%d" % i, [128, 512], F32) for i in range(2)]
        self.rstd = P.sb(t + "rstd", [128, 512], F32)
        self.rl = [P.sb(t + "rl%d" % i, [128, 512], F32) for i in range(2)]
        self.xT = [carve(4096).rearrange("p (k n) -> p k n", n=512) for _ in range(2)]
        self.hTn = carve(2048).bitcast(BF16).rearrange("p (k n) -> p k n", n=512)
        mh = carve(4096)
        self.tmj = mh.rearrange("p (s n) -> p s n", n=1024)
        self.mixt = mh[:, 0:2048].bitcast(BF16).rearrange("p (k n) -> p k n", n=512)
        self.h2T = mh[:, 2048:4096].bitcast(BF16).rearrange("p (k n) -> p k n", n=512)
        self.uT = carve(8192).bitcast(BF16).rearrange("p (k n) -> p k n", n=512)
        self.Woutb = carve(4096).bitcast(BF16).rearrange("p (k n) -> p k n", n=1024)
        self.W1t = [carve(2048).bitcast(BF16).rearrange("p (k n) -> p k n", n=512) for _ in range(2)]
        self.W2t = [carve(2048).bitcast(BF16).rearrange("p (k n) -> p k n", n=128) for _ in range(2)]
        self.ldx = [P.dma_sem(t + "ldx%d" % i) for i in range(2)]
        self.stx = [P.dma_sem(t + "stx%d" % i) for i in range(2)]
        self.ldg = P.dma_sem(t + "ldg")
        self.st = P.dma_sem(t + "st")
        self.ldm = P.dma_sem(t + "ldm")
        self.ldw1 = [P.dma_sem(t + "ldw1_%d" % i) for i in range(2)]
        self.ldw2 = [P.dma_sem(t + "ldw2_%d" % i) for i in range(2)]
        self.cst = P.dma_sem(t + "cst")
        self.ldwo = P.dma_sem(t + "ldwo")
        self.sto = P.dma_sem(t + "sto")
        self.nb = 0
        self.nw1 = 0
        self.nw2 = 0

    def k(self, *a):
        return (self.tag,) + a

    def bank(self):
        i = self.nb % 8
        self.nb += 1
        key = ("A", "bank", i)
        self.P.regroup(key)
        return self.banks[i], key

    def cast_weights(self, w_ff1, w_ff2, w1b, w2b, l):
        P = self.P
        for i in range(4):
            s1 = w_ff1.rearrange("(a p) n -> p a n", p=128)[:, 2 * i:2 * i + 2, :]
            d1 = w1b.rearrange("(a p) n -> p a n", p=128)[:, 2 * i:2 * i + 2, :]
            P.op("pool", lambda e: e.dma_start(out=d1, in_=s1), w=[("w1b", l), "castchain"], dsem=self.cst)
        for f in range(32):
            s2 = w_ff2[f * 128:(f + 1) * 128, :].rearrange("p (d m) -> p d m", m=128)
            d2 = w2b[:, :, f, :].rearrange("d p m -> p d m")
            P.op("pool", lambda e: e.dma_start(out=d2, in_=s2), w=[("w2b", l), "castchain"], dsem=self.cst)

    def norm(self, xt, xkeys, gcol0, dst, dkey):
        P, K = self.P, self.k
        ones = self.gp[:, 16:144]
        bk, bkk = self.bank()
        for kc in range(8):
            sq = self.sq[kc % 2]
            P.op("act", lambda e: e.activation(out=sq[:], in_=xt[:, kc, :], func=AF.Square),
                 r=[xkeys[kc]], w=[K("sq", kc % 2)])
            P.op("pe", lambda e: e.matmul(bk[:], ones, sq[:], start=(kc == 0), stop=(kc == 7)),
                 r=[K("sq", kc % 2), K("gp")], w=[bkk])
        P.op("act", lambda e: e.activation(out=self.rstd[:], in_=bk[:], func=AF.Sqrt, scale=1.0 / D_MODEL,
                                           bias=self.small[:, 0:1]),
             r=[bkk, K("eps")], w=[K("rstd")])
        P.op("dve", lambda e: e.reciprocal(self.rstd[:], self.rstd[:]), r=[K("rstd")], w=[K("rstd")])
        for kc in range(8):
            P.op("dve", lambda e: e.scalar_tensor_tensor(dst[:, kc, :], xt[:, kc, :],
                                                         self.gp[:, gcol0 + kc:gcol0 + kc + 1], self.rstd[:],
                                                         ALU.mult, ALU.mult),
                 r=[xkeys[kc], K("rstd"), K("gp")], w=[dkey])

    def run(self, mode, gpar, xTd, hTd, x_in=None, out_d=None, mixd=None, w_out=None, w1b=None, w2b=None, l=0):
        P, K = self.P, self.k
        ident = self.gp[:, 144:272]
        P.op("pool", lambda e: e.memset(self.small[:, 0:1], EPS), w=[K("eps")])
        P.op("sp", lambda e: e.dma_start(out=self.gp[:], in_=gpar), w=[K("gp")], dsem=self.ldg)
        if mode != "pro":
            src = w_out.rearrange("(kc p) n -> p kc n", p=128)
            P.op("pool", lambda e: e.dma_start(out=self.Woutb, in_=src), w=[K("Woutb")], dsem=self.ldwo)
        ncp = 0
        for it in range(self.nt):
            ts = slice(it * 512, (it + 1) * 512)
            xb = it % 2
            xt = self.xT[xb]
            xk = [K("xT", xb, kc) for kc in range(8)]
            if mode == "pro":
                srcx = x_in.rearrange("(s p) n -> p s n", p=128)[:, it * 4:(it + 1) * 4, :]
                P.op("sp", lambda e: e.dma_start(out=self.tmj, in_=srcx), w=[K("mixt"), K("h2T")], dsem=self.ldm)
                for kc in range(8):
                    bk, bkk = self.bank()
                    for sub in range(4):
                        P.op("pe", lambda e: e.transpose(bk[:, sub * 128:(sub + 1) * 128],
                                                         self.tmj[:, sub, kc * 128:(kc + 1) * 128], ident),
                             r=[K("mixt"), K("h2T"), K("gp")], w=[bkk])
                    if ncp % 2 == 0:
                        P.op("act", lambda e: e.copy(xt[:, kc, :], bk[:]), r=[bkk], w=[xk[kc]])
                    else:
                        P.op("dve", lambda e: e.tensor_copy(xt[:, kc, :], bk[:]), r=[bkk], w=[xk[kc]])
                    ncp += 1
            else:
                src = xTd.rearrange("(kc p) t -> p kc t", p=128)[:, :, ts]
                P.op("sp", lambda e: e.dma_start(out=xt, in_=src), w=xk, dsem=self.ldx[xb])
                srcm = mixd.rearrange("(kc p) t -> p kc t", p=128)[:, :, ts]
                P.op("sp", lambda e: e.dma_start(out=self.mixt, in_=srcm), w=[K("mixt")], dsem=self.ldm)
                for dmc in range(8):
                    bk, bkk = self.bank()
                    for kc in range(8):
                        P.op("pe", lambda e: e.matmul(bk[:], self.Woutb[:, kc, dmc * 128:(dmc + 1) * 128],
                                                      self.mixt[:, kc, :], start=(kc == 0), stop=(kc == 7)),
                             r=[K("Woutb"), K("mixt")], w=[bkk])
                    P.op("dve", lambda e: e.tensor_tensor(xt[:, dmc, :], xt[:, dmc, :], bk[:], ALU.add),
                         r=[bkk, xk[dmc]], w=[xk[dmc]])
                self.norm(xt, xk, 0, self.h2T, K("h2T"))
                for fg in range(8):
                    wb = self.nw1 % 2
                    self.nw1 += 1
                    s1 = w1b.rearrange("(kc p) n -> p kc n", p=128)[:, :, fg * 512:(fg + 1) * 512]
                    P.op("sp", lambda e: e.dma_start(out=self.W1t[wb], in_=s1), r=[("w1b", l)], w=[K("W1t", wb)],
                         dsem=self.ldw1[wb])
                    for jj in range(4):
                        ffc = fg * 4 + jj
                        bk, bkk = self.bank()
                        for kc in range(8):
                            P.op("pe", lambda e: e.matmul(bk[:], self.W1t[wb][:, kc, jj * 128:(jj + 1) * 128],
                                                          self.h2T[:, kc, :], start=(kc == 0), stop=(kc == 7)),
                                 r=[K("W1t", wb), K("h2T")], w=[bkk])
                        rl = self.rl[ffc % 2]
                        P.op("act", lambda e: e.activation(out=rl[:], in_=bk[:], func=AF.Relu),
                             r=[bkk], w=[K("rl", ffc % 2)])
                        P.op("dve", lambda e: e.tensor_tensor(self.uT[:, ffc, :], rl[:], bk[:], ALU.mult),
                             r=[bkk, K("rl", ffc % 2)], w=[K("uT", ffc)])
                for dmc in range(8):
                    wb = self.nw2 % 2
                    self.nw2 += 1
                    P.op("sp", lambda e: e.dma_start(out=self.W2t[wb], in_=w2b[dmc]), r=[("w2b", l)],
                         w=[K("W2t", wb)], dsem=self.ldw2[wb])
                    bk, bkk = self.bank()
                    for ffc in range(32):
                        P.op("pe", lambda e: e.matmul(bk[:], self.W2t[wb][:, ffc, :], self.uT[:, ffc, :],
                                                      start=(ffc == 0), stop=(ffc == 31)),
                             r=[K("W2t", wb), K("uT", ffc)], w=[bkk])
                    P.op("dve", lambda e: e.tensor_tensor(xt[:, dmc, :], xt[:, dmc, :], bk[:], ALU.add),
                         r=[bkk, xk[dmc]], w=[xk[dmc]])
            if mode == "last":
                for sub in range(4):
                    for half in range(2):
                        bk, bkk = self.bank()
                        for kk in range(4):
                            kc = half * 4 + kk
                            P.op("pe", lambda e: e.transpose(bk[:, kk * 128:(kk + 1) * 128],
                                                             xt[:, kc, sub * 128:(sub + 1) * 128], ident),
                                 r=[xk[kc], K("gp")], w=[bkk])
                        if ncp % 2 == 0:
                            P.op("act", lambda e: e.copy(self.tmj[:, sub, half * 512:(half + 1) * 512], bk[:]),
                                 r=[bkk, K("mixt"), K("h2T")], w=[K("tmj", sub, half)])
                        else:
                            P.op("dve", lambda e: e.tensor_copy(self.tmj[:, sub, half * 512:(half + 1) * 512], bk[:]),
                                 r=[bkk, K("mixt"), K("h2T")], w=[K("tmj", sub, half)])
                        ncp += 1
                dsto = out_d.rearrange("(s p) n -> p s n", p=128)[:, it * 4:(it + 1) * 4, :]
                P.op("sp", lambda e: e.dma_start(out=dsto, in_=self.tmj),
                     r=[K("tmj", s_, h_) for s_ in range(4) for h_ in range(2)], w=[K("mixt"), K("h2T")],
                     dsem=self.sto)
            else:
                dstx = xTd.rearrange("(kc p) t -> p kc t", p=128)[:, :, ts]
                P.op("sp", lambda e: e.dma_start(out=dstx, in_=xt), r=xk, dsem=self.stx[xb])
                self.norm(xt, xk, 8, self.hTn, K("hTn"))
                dsth = hTd.rearrange("(kc p) t -> p kc t", p=128)[:, :, ts]
                P.op("sp", lambda e: e.dma_start(out=dsth, in_=self.hTn), r=[K("hTn")], dsem=self.st)


OV32 = 38192


def build_fused(depth=DEPTH, nt=NT):
    nc = bass.Bass("TRN2", target_bir_lowering=False)
    P = Prog(nc)
    ntok = nt * 512
    x = nc.dram_tensor("x", [SEQ, D_MODEL], F32, kind="ExternalInput").ap()
    wA = nc.dram_tensor("wA", [DEPTH * 2 * D_MODEL, 1800], F32, kind="ExternalInput").ap()
    par = nc.dram_tensor("par", [DEPTH * 2 * 128, PL.n], F32, kind="ExternalInput").ap()
    bt = nc.dram_tensor("bt", [2 * 128, 2, 6, 512], F32, kind="ExternalInput").ap()
    w_out = nc.dram_tensor("w_out", [DEPTH * D_MODEL, D_MODEL], F32, kind="ExternalInput").ap()
    w_ff1 = nc.dram_tensor("w_ff1", [DEPTH * D_MODEL, D_FF], F32, kind="ExternalInput").ap()
    w_ff2 = nc.dram_tensor("w_ff2", [DEPTH * D_FF, D_MODEL], F32, kind="ExternalInput").ap()
    gpar = nc.dram_tensor("gpar", [(DEPTH + 1) * 128, GPW], F32, kind="ExternalInput").ap()
    out = nc.dram_tensor("out", [SEQ, D_MODEL], F32, kind="ExternalOutput").ap()
    xTd = nc.dram_tensor("xTd", [D_MODEL, SEQ], F32).ap()
    hTd = nc.dram_tensor("hTd", [D_MODEL, SEQ], BF16).ap()
    mixd = nc.dram_tensor("mixd", [D_MODEL, SEQ], BF16).ap()
    w1b = [nc.dram_tensor("w1b%d" % l, [D_MODEL, D_FF], BF16).ap() for l in range(depth)]
    w2b = [nc.dram_tensor("w2b%d" % l, [8, 128, 32, 128], BF16).ap() for l in range(depth)]
    ov = P.sb("ov", [128, OV32], F32)
    offs = {"A": 0, "B": 0}

    def carver(which):
        def carve(n32):
            o = offs[which]
            offs[which] = o + n32
            assert offs[which] <= OV32
            return ov[:, o:o + n32]
        return carve

    A = PhaseA(P, None, None, None, None, None, tag="A", carve=carver("A"))
    B = PhaseB2(P, carver("B"), A.banks, nt)
    init_small(P, A.small, A.k("eps"))
    for l in range(depth):
        B.cast_weights(w_ff1[l * D_MODEL:(l + 1) * D_MODEL, :], w_ff2[l * D_FF:(l + 1) * D_FF, :], w1b[l], w2b[l], l)
    B.run("pro", gpar[0:128, :], xTd, hTd, x_in=x)
    P.barrier()
    for l in range(depth):
        for g in range(2):
            r0 = (l * 2 + g) * D_MODEL
            p0 = (l * 2 + g) * 128
            A.bind(hTd, wA[r0:r0 + D_MODEL, :], par[p0:p0 + 128, :], bt[g * 128:(g + 1) * 128], mixd,
                   256 * g, 512 + 256 * g)
            A.run()
            P.barrier()
        B.run("last" if l == depth - 1 else "mid", gpar[(l + 1) * 128:(l + 2) * 128, :], xTd, hTd, out_d=out,
              mixd=mixd, w_out=w_out[l * D_MODEL:(l + 1) * D_MODEL, :], w1b=w1b[l], w2b=w2b[l], l=l)
        P.barrier()
    P.emit()
    return nc


def fused_inputs(inp, b):
    wA = np.concatenate([build_wA(inp, l, g) for l in range(DEPTH) for g in range(2)], axis=0)
    par = np.concatenate([build_par(inp, l, g) for l in range(DEPTH) for g in range(2)], axis=0)
    bt = np.concatenate([build_bt(inp, g) for g in range(2)], axis=0)
    gps = [build_gpar2(inp["norm1_g"][0], inp["norm1_g"][0])]
    for l in range(DEPTH):
        gn = inp["norm1_g"][l + 1] if l + 1 < DEPTH else inp["norm1_g"][0]
        gps.append(build_gpar2(inp["norm2_g"][l], gn))
    return {"x": np.ascontiguousarray(inp["x"][b]), "wA": wA, "par": par, "bt": bt,
            "w_out": inp["w_out"].reshape(DEPTH * D_MODEL, D_MODEL),
            "w_ff1": inp["w_ff1"].reshape(DEPTH * D_MODEL, D_FF),
            "w_ff2": inp["w_ff2"].reshape(DEPTH * D_FF, D_MODEL),
            "gpar": np.concatenate(gps, axis=0)}


def kernel(**inputs):
    inp = {k: np.asarray(v) for k, v in inputs.items()}
    nc = build_fused()
    shared = fused_inputs(inp, 0)
    in_maps = []
    for c in range(8):
        m = dict(shared)
        m["x"] = np.ascontiguousarray(inp["x"][c % BATCH])
        in_maps.append(m)
    res = run_bass_kernel_spmd(nc, in_maps, core_ids=list(range(8)))
    return np.stack([np.asarray(res.results[b]["out"]) for b in range(BATCH)], axis=0).astype(np.float32)


_NC_CACHE = {}


def _get_nc(name):
    if name not in _NC_CACHE:
        if name == "A":
            _NC_CACHE[name] = build_A()
        elif name == "B":
            _NC_CACHE[name] = build_B("full")
        else:
            _NC_CACHE[name] = build_B("norm")
    return _NC_CACHE[name]


def _run(name, in_maps):
    if name == "A":
        nc = build_A()
    elif name == "B":
        nc = build_B("full")
    else:
        nc = build_B("norm")
    res = run_bass_kernel_spmd(nc, in_maps, core_ids=list(range(8)))
    return res.results


def kernel_unfused(**inputs):
    inp = {k: np.asarray(v) for k, v in inputs.items()}
    x = inp["x"].astype(np.float32, copy=False)
    ncore = 8
    xT = []
    for c in range(ncore):
        b, g = c // 2, c % 2
        xT.append(np.ascontiguousarray(x[b, g * HALF:(g + 1) * HALF].T))
    gp0 = build_gpar(inp["norm1_g"][0], inp["norm1_g"][0])
    res = _run("N", [{"xT_in": xT[c], "gpar": gp0} for c in range(ncore)])
    hT_half = [np.asarray(res[c]["hT_next"]) for c in range(ncore)]
    for l in range(DEPTH):
        in_maps = []
        for c in range(ncore):
            b, g = c // 2, c % 2
            hT_full = np.ascontiguousarray(np.concatenate([hT_half[2 * b], hT_half[2 * b + 1]], axis=1))
            in_maps.append({"hT": hT_full, "wA": build_wA(inp, l, g), "par": build_par(inp, l, g),
                            "bt": build_bt(inp, g)})
        resA = _run("A", in_maps)
        mix = [np.asarray(resA[c]["mixT"]) for c in range(ncore)]
        gnext = inp["norm1_g"][l + 1] if l + 1 < DEPTH else inp["norm1_g"][0]
        gp = build_gpar(inp["norm2_g"][l], gnext)
        in_maps = []
        for c in range(ncore):
            b, g = c // 2, c % 2
            m0, m1 = mix[2 * b], mix[2 * b + 1]
            ts = slice(g * HALF, (g + 1) * HALF)
            mall = np.ascontiguousarray(np.concatenate([m0[0:256, ts], m1[0:256, ts], m0[256:512, ts], m1[256:512, ts]],
                                                       axis=0))
            in_maps.append({"xT_in": xT[c], "mixT_all": mall, "w_out": inp["w_out"][l], "w_ff1": inp["w_ff1"][l],
                            "w_ff2": inp["w_ff2"][l], "gpar": gp})
        resB = _run("B", in_maps)
        xT = [np.asarray(resB[c]["xT_out"]) for c in range(ncore)]
        hT_half = [np.asarray(resB[c]["hT_next"]) for c in range(ncore)]
    out = np.empty((BATCH, SEQ, D_MODEL), np.float32)
    for c in range(ncore):
        b, g = c // 2, c % 2
        out[b, g * HALF:(g + 1) * HALF] = xT[c].T
    return out
```
